# Optimizing a Trainium2 kernel written in Bass

```python
import math
import jax, jax.numpy as jnp
from jax import lax
import numpy as np

D_MODEL = 2048
BATCH = 4
SEQ = 4096
DEPTH = 4

N_META = 16
CHUNK = 128
PAD_FRONT = CHUNK - N_META
A_HEADS = D_MODEL // 256
A_QK_DIM = 64
A_V_DIM = 128
D_A = A_HEADS * A_V_DIM
R_HEADS = D_MODEL // 512
R_QK_DIM = 128
R_V_DIM = 256
D_R = R_HEADS * R_V_DIM
D_MIX = D_A + D_R
COL_SIZES = (A_HEADS * 2 * A_QK_DIM, A_HEADS * 2 * A_QK_DIM, D_A,
             R_HEADS * R_QK_DIM, R_HEADS * R_QK_DIM, D_R, D_R)
SPLITS = tuple(int(s) for s in np.cumsum(COL_SIZES)[:-1])
D_IN = int(sum(COL_SIZES))
D_FF = 256 * math.ceil(8 * D_MODEL / 3 / 256)
CONV_W = 3
EPS = 1e-6
NEG_INF = -1e30

kernel_name = "hymba_diffattn_retnet_convffn"


def rmsnorm(x, g):
    xf = x.astype(jnp.float32)
    y = xf * lax.rsqrt(jnp.mean(xf * xf, axis=-1, keepdims=True) + EPS)
    return (y * g.astype(jnp.float32)).astype(x.dtype)


def diff_attention(q, k, v, lam, slopes, valid):
    B, L, H, _, dqk = q.shape
    dv = v.shape[-1]
    scale = dqk ** -0.5
    kpos = jnp.arange(L)

    def block(start):
        qb = lax.dynamic_slice_in_dim(q, start, CHUNK, axis=1)
        s = jnp.einsum('bqhmd,bkhmd->bhmqk', qb, k).astype(jnp.float32) * scale
        qpos = start + jnp.arange(CHUNK)
        dist = (qpos[:, None] - kpos[None, :])
        bias = -slopes[:, None, None] * dist.astype(jnp.float32)
        mask = (dist >= 0) & valid[None, :]
        s = jnp.where(mask[None, None, None], s + bias[None, :, None], NEG_INF)
        p = jax.nn.softmax(s, axis=-1)
        a = (p[:, :, 0] - lam * p[:, :, 1]).astype(v.dtype)
        return jnp.einsum('bhqk,bkhe->bqhe', a, v)

    starts = jnp.arange(L // CHUNK) * CHUNK
    out = lax.map(block, starts)
    return jnp.transpose(out, (1, 0, 2, 3, 4)).reshape(B, L, H, dv)


def retention(q, k, v, log_g):
    B, L, H, dk = q.shape
    dv = v.shape[-1]
    N = L // CHUNK
    dt = q.dtype
    q = q.reshape(B, N, CHUNK, H, dk)
    k = k.reshape(B, N, CHUNK, H, dk) * (dk ** -0.5)
    v = v.reshape(B, N, CHUNK, H, dv)
    idx = jnp.arange(CHUNK, dtype=jnp.float32)
    diff = idx[:, None] - idx[None, :]
    decay_in = jnp.where(diff >= 0, jnp.exp(log_g[:, None, None] * jnp.maximum(diff, 0.0)), 0.0)
    s = jnp.einsum('bnihd,bnjhd->bnhij', q, k) * decay_in.astype(dt)[None, None]
    intra = jnp.einsum('bnhij,bnjhe->bnihe', s, v)
    k_dec = k * jnp.exp(log_g[None, :] * (CHUNK - 1 - idx)[:, None]).astype(dt)[None, None, :, :, None]
    kv = jnp.einsum('bnjhd,bnjhe->nbhde', k_dec, v)
    g_chunk = jnp.exp(log_g * CHUNK).astype(kv.dtype)[None, :, None, None]

    def step(S, kv_n):
        return g_chunk * S + kv_n, S

    _, S_prev = lax.scan(step, jnp.zeros((B, H, dk, dv), kv.dtype), kv)
    q_dec = q * jnp.exp(log_g[None, :] * (idx + 1.0)[:, None]).astype(dt)[None, None, :, :, None]
    cross = jnp.einsum('bnihd,nbhde->bnihe', q_dec, S_prev)
    return (intra + cross).reshape(B, L, H, dv)


def conv_glu(u, w_gate, w_up, conv_w, w_down):
    L = u.shape[1]
    g = u @ w_gate
    gp = jnp.pad(g, ((0, 0), (CONV_W - 1, 0), (0, 0)))
    gc = gp[:, 0:L] * conv_w[0]
    for i in range(1, CONV_W):
        gc = gc + gp[:, i:i + L] * conv_w[i]
    return (jax.nn.silu(gc) * (u @ w_up)) @ w_down


def setup_inputs(seed: int = 0) -> dict:
    key = jax.random.key(seed)
    ks = jax.random.split(key, 14)
    f32 = jnp.float32
    nrm = lambda k, shape, s: jax.random.normal(k, shape, f32) * s
    return {
        "x": nrm(ks[0], (BATCH, SEQ, D_MODEL), 1.0),
        "meta_tokens": nrm(ks[1], (N_META, D_MODEL), 1.0),
        "attn_norm": 1.0 + nrm(ks[2], (DEPTH, D_MODEL), 0.02),
        "w_in": nrm(ks[3], (DEPTH, D_MODEL, D_IN), D_MODEL ** -0.5),
        "lambda_qk": nrm(ks[4], (DEPTH, 4, A_QK_DIM), 0.1),
        "attn_subln": 1.0 + nrm(ks[5], (DEPTH, A_V_DIM), 0.02),
        "ret_norm": 1.0 + nrm(ks[6], (DEPTH, R_V_DIM), 0.02),
        "w_out": nrm(ks[7], (DEPTH, D_MIX, D_MODEL), D_MIX ** -0.5),
        "ffn_norm": 1.0 + nrm(ks[8], (DEPTH, D_MODEL), 0.02),
        "w_gate": nrm(ks[9], (DEPTH, D_MODEL, D_FF), D_MODEL ** -0.5),
        "w_up": nrm(ks[10], (DEPTH, D_MODEL, D_FF), D_MODEL ** -0.5),
        "conv_w": nrm(ks[11], (DEPTH, CONV_W, D_FF), CONV_W ** -0.5),
        "w_down": nrm(ks[12], (DEPTH, D_FF, D_MODEL), D_FF ** -0.5),
        "final_norm": 1.0 + nrm(ks[13], (D_MODEL,), 0.02),
    }


def reference(x, meta_tokens, attn_norm, w_in, lambda_qk, attn_subln, ret_norm, w_out,
              ffn_norm, w_gate, w_up, conv_w, w_down, final_norm):
    B, S, D = x.shape
    L = CHUNK + S
    pad = jnp.zeros((B, PAD_FRONT, D), x.dtype)
    meta = jnp.broadcast_to(meta_tokens.astype(x.dtype)[None], (B, N_META, D))
    h = jnp.concatenate([pad, meta, x], axis=1)
    valid = jnp.arange(L) >= PAD_FRONT
    vmask = valid.astype(x.dtype)[None, :, None]
    slopes = 2.0 ** (-8.0 * jnp.arange(1, A_HEADS + 1, dtype=jnp.float32) / A_HEADS)
    log_g = jnp.log1p(-(2.0 ** (-5.0 - jnp.arange(R_HEADS, dtype=jnp.float32))))

    for l in range(DEPTH):
        lam_init = 0.8 - 0.6 * math.exp(-0.3 * l)
        u = rmsnorm(h, attn_norm[l]) * vmask
        proj = u @ w_in[l]
        aq, ak, av, rq, rk, rv, rg = jnp.split(proj, SPLITS, axis=-1)
        lq = lambda_qk[l].astype(jnp.float32)
        lam = jnp.exp(jnp.sum(lq[0] * lq[1])) - jnp.exp(jnp.sum(lq[2] * lq[3])) + lam_init
        ya = diff_attention(aq.reshape(B, L, A_HEADS, 2, A_QK_DIM),
                            ak.reshape(B, L, A_HEADS, 2, A_QK_DIM),
                            av.reshape(B, L, A_HEADS, A_V_DIM), lam, slopes, valid)
        ya = rmsnorm(ya, attn_subln[l]) * (1.0 - lam_init)
        yr = retention(rq.reshape(B, L, R_HEADS, R_QK_DIM),
                       rk.reshape(B, L, R_HEADS, R_QK_DIM),
                       rv.reshape(B, L, R_HEADS, R_V_DIM), log_g)
        yr = rmsnorm(yr, ret_norm[l]) * jax.nn.silu(rg.reshape(B, L, R_HEADS, R_V_DIM))
        y = jnp.concatenate([ya.reshape(B, L, D_A), yr.reshape(B, L, D_R)], axis=-1)
        h = h + y @ w_out[l]
        u = rmsnorm(h, ffn_norm[l]) * vmask
        h = h + conv_glu(u, w_gate[l], w_up[l], conv_w[l], w_down[l])

    out = rmsnorm(h, final_norm)
    return out[:, CHUNK:]
```

```python
import math
from contextlib import ExitStack
import numpy as np
import concourse.bass as bass
import concourse.mybir as mybir
from concourse.bass_utils import run_bass_kernel_spmd

F32 = mybir.dt.float32
BF16 = mybir.dt.bfloat16
AF = mybir.ActivationFunctionType
ALU = mybir.AluOpType
EPS = 1e-6
CH = 128


class Cfg:
    def __init__(self, D=2048, NT=33, AH=8, RH=4, DFF=5632, DEPTH=4, NPADF=112):
        self.D, self.NT, self.AH, self.RH, self.DFF, self.DEPTH = D, NT, AH, RH, DFF, DEPTH
        self.L = NT * 128
        self.KT = D // 128
        self.FT = DFF // 128
        self.DA = AH * 128
        self.DR = RH * 256
        self.DMIX = self.DA + self.DR
        self.MT = self.DMIX // 128
        self.NPADF = NPADF
        self.c_aq = 0
        self.c_ak = self.DA
        self.c_av = 2 * self.DA
        self.c_rq = 3 * self.DA
        self.c_rk = self.c_rq + RH * 128
        self.c_rv = self.c_rk + RH * 128
        self.c_rg = self.c_rv + self.DR
        self.DIN = self.c_rg + self.DR
        self.slopes = [2.0 ** (-8.0 * (h + 1) / AH) for h in range(AH)]
        self.log_g = [math.log1p(-(2.0 ** (-5.0 - r))) for r in range(RH)]


class Prog:
    ENGS = ["tensor", "vector", "scalar", "gpsimd", "sync"]

    def __init__(self, nc):
        self.nc = nc
        self.ops = {e: [] for e in self.ENGS}
        self.csem = {e: nc.alloc_semaphore("c_" + e) for e in ["tensor", "vector", "scalar", "gpsimd"]}
        self.ccnt = {e: 0 for e in self.csem}
        self.dcnt = {}
        self.dsems = []
        self.waited = {}
        self.nsem = 0
        self.semcache = {}

    def new_epoch(self):
        self.epoch = getattr(self, "epoch", 0) + 1
        self.csem = {e: self.nc.alloc_semaphore(f"c{self.epoch}_{e}") for e in ["tensor", "vector", "scalar", "gpsimd"]}
        self.ccnt = {e: 0 for e in self.csem}

    def emit(self, eng, fn):
        op = [fn, None]
        self.ops[eng].append(op)
        return op

    def signal(self, eng):
        k = len(self.ops[eng]) - 1
        while self.ops[eng][k][0] is None:
            k -= 1
        op = self.ops[eng][k]
        if op[1] is None:
            self.ccnt[eng] += 1
            op[1] = (self.csem[eng], 1)
        assert op[1][0] is self.csem[eng]
        return (self.csem[eng], self.ccnt[eng])

    def wait(self, eng, ev):
        if ev is None:
            return
        if isinstance(ev, list):
            for x in ev:
                self.wait(eng, x)
            return
        sem, val = ev
        key = (eng, id(sem))
        if self.waited.get(key, 0) >= val:
            return
        self.waited[key] = val
        self.ops[eng].append([None, (sem, val)])

    def newsem(self, name="d"):
        if name in self.semcache:
            return self.semcache[name]
        self.nsem += 1
        s = self.nc.alloc_semaphore(f"{name}_{self.nsem}")
        self.dcnt[id(s)] = 0
        self.dsems.append(s)
        self.semcache[name] = s
        return s

    def dma(self, eng, sem, out, in_):
        op = self.emit(eng, lambda e: e.dma_start(out=out, in_=in_))
        self.dcnt[id(sem)] += 16
        op[1] = (sem, 16)
        return (sem, self.dcnt[id(sem)])

    def barrier(self):
        evs = [(s, self.dcnt[id(s)]) for s in self.dsems if self.dcnt[id(s)] > 0]
        evs += [(self.csem[e], self.ccnt[e]) for e in self.csem if self.ccnt[e] > 0]
        for eng in self.ENGS:
            for ev in evs:
                self.wait(eng, ev)

    def run(self):
        nc = self.nc
        with nc.Block() as block:
            for ename in self.ENGS:
                ops = self.ops[ename]

                def body(e, ops=ops):
                    for fn, x in ops:
                        if fn is None:
                            e.wait_ge(x[0], x[1])
                        else:
                            ins = fn(e)
                            if x is not None:
                                ins.then_inc(x[0], x[1])

                getattr(block, ename)(body)


class Ring:
    def __init__(self, bufs):
        self.bufs = bufs
        self.free = [None] * len(bufs)
        self.i = -1

    def next(self):
        self.i = (self.i + 1) % len(self.bufs)
        return self.i, self.bufs[self.i], self.free[self.i]

    def release(self, i, ev):
        self.free[i] = ev


def split_blocks(n, mx):
    out = []
    c = 0
    while c < n:
        w = min(mx, n - c)
        out.append((c, w))
        c += w
    return out


class StopBuild(Exception):
    pass


def build_program(cfg: Cfg, stop=None):
    nc = bass.Bass("TRN2", target_bir_lowering=False)
    D, L, NT, KT, FT, AH, RH, DEPTH = cfg.D, cfg.L, cfg.NT, cfg.KT, cfg.FT, cfg.AH, cfg.RH, cfg.DEPTH
    DA, DR, DMIX, MT, DFF, DIN = cfg.DA, cfg.DR, cfg.DMIX, cfg.MT, cfg.DFF, cfg.DIN
    P = Prog(nc)

    def din(name, shape, dt=F32):
        return nc.dram_tensor(name, list(shape), dt, kind="ExternalInput").ap()

    def dscr(name, shape, dt):
        return nc.dram_tensor(name, list(shape), dt, kind="Internal").ap()

    h0T = din("h0T", [D, L])
    w_in = din("w_in", [DEPTH, D, DIN])
    w_out = din("w_out", [DEPTH, DMIX, D])
    w_gate = din("w_gate", [DEPTH, D, DFF])
    w_up = din("w_up", [DEPTH, D, DFF])
    w_down = din("w_down", [DEPTH, DFF, D])
    gains_d = din("gains", [128, (2 * DEPTH + 1) * KT])
    convw_d = din("convw", [128, DEPTH * 3 * FT])
    subln_d = din("subln", [128, DEPTH * 128])
    retn_d = din("retn", [128, DEPTH * 256])
    lamqk_d = din("lamqk", [128, DEPTH * 256])
    ident_d = din("ident", [128, 128])
    tri_d = din("tri", [128, 256])
    biasT_d = din("biasT", [128, 2 * AH * NT])
    retM_d = din("retM", [128, RH * 128])
    gq_d = din("gq", [128, RH * 512])
    kdec_d = din("kdec", [128, RH * 128])
    gC_d = din("gC", [128, RH])
    outT = nc.dram_tensor("outT", [D, L - 128], F32, kind="ExternalOutput").ap()

    hT = dscr("hT", [D, L], F32)
    qaT = dscr("qaT", [DA, L], BF16)
    kaT = dscr("kaT", [DA, L], BF16)
    va = dscr("va", [L, DA], BF16)
    rqT = dscr("rqT", [RH * 128, L], BF16)
    rkT = dscr("rkT", [RH * 128, L], BF16)
    rkd = dscr("rkd", [L, RH * 128], BF16)
    rv = dscr("rv", [L, DR], BF16)
    rgs = dscr("rgs", [L, DR], F32)
    ytm = dscr("ytm", [L, DMIX], BF16)
    yT = dscr("yT", [DMIX, L], BF16)
    aT = dscr("aT", [DFF, L], BF16)

    stacks = [ExitStack()]

    def sb(name, shape, dt):
        return stacks[-1].enter_context(nc.sbuf_tensor("s_" + name, list(shape), dt)).ap()

    def phase_begin():
        stacks.append(ExitStack())

    def phase_end():
        P.barrier()
        stacks.pop().close()

    gains = sb("gains", [128, (2 * DEPTH + 1) * KT], F32)
    convw = sb("convw", [128, DEPTH * 3 * FT], F32)
    subln = sb("subln", [128, DEPTH * 128], F32)
    retn = sb("retn", [128, DEPTH * 256], F32)
    lamqk = sb("lamqk", [128, DEPTH * 256], F32)
    ident = sb("ident", [128, 128], BF16)
    tri = sb("tri", [128, 256], BF16)
    biasT = sb("biasT", [128, 2 * AH * NT], F32)
    retM = sb("retM", [128, RH * 128], F32)
    gq = sb("gq", [128, RH * 512], F32)
    kdec = sb("kdec", [128, RH * 128], F32)
    gC = sb("gC", [128, RH], F32)
    ones_bf = sb("ones_bf", [128, 128], BF16)
    lam_t = sb("lam_t", [128, 4 * DEPTH], F32)
    lam_scr = sb("lam_scr", [128, 64], F32)

    csem = P.newsem("const")
    evs = []
    for dst, src in [(gains, gains_d), (convw, convw_d), (subln, subln_d), (retn, retn_d), (lamqk, lamqk_d),
                     (biasT, biasT_d), (retM, retM_d), (gq, gq_d), (kdec, kdec_d), (gC, gC_d)]:
        evs.append(P.dma("sync", csem, dst, src))
    csem2 = P.newsem("const2")
    evs.append(P.dma("gpsimd", csem2, ident, ident_d))
    evs.append(P.dma("gpsimd", csem2, tri, tri_d))
    ev_const = evs[-1]
    epsb = sb("epsb", [128, 1], F32)
    P.emit("vector", lambda e: e.memset(epsb, EPS))
    P.emit("vector", lambda e: e.memset(ones_bf, 1.0))
    ev_ones = P.signal("vector")
    P.barrier()
    for l in range(DEPTH):
        lam_init = 0.8 - 0.6 * math.exp(-0.3 * l)
        lq = lamqk[:, l * 256:(l + 1) * 256]
        for i in range(2):
            a = lq[:, (2 * i) * 64:(2 * i + 1) * 64]
            b = lq[:, (2 * i + 1) * 64:(2 * i + 2) * 64]
            P.emit("vector", lambda e, a=a, b=b: e.tensor_tensor(out=lam_scr, in0=a, in1=b, op=ALU.mult))
            ev = P.signal("vector")
            P.wait("vector", ev)
            dst = lam_t[:, 4 * l + 2 + i:4 * l + 3 + i]
            P.emit("vector", lambda e, dst=dst: e.reduce_sum(out=dst, in_=lam_scr, axis=mybir.AxisListType.X))
            ev = P.signal("vector")
            P.wait("scalar", ev)
            P.emit("scalar", lambda e, dst=dst: e.activation(out=dst, in_=dst, func=AF.Exp))
            ev = P.signal("scalar")
            P.wait("vector", ev)
        la = lam_t[:, 4 * l:4 * l + 1]
        nla = lam_t[:, 4 * l + 1:4 * l + 2]
        e0 = lam_t[:, 4 * l + 2:4 * l + 3]
        e1 = lam_t[:, 4 * l + 3:4 * l + 4]
        P.emit("vector", lambda e, la=la, e0=e0, e1=e1: e.tensor_tensor(out=la, in0=e0, in1=e1, op=ALU.subtract))
        ev = P.signal("vector"); P.wait("vector", ev)
        P.emit("vector", lambda e, la=la, li=lam_init: e.tensor_scalar(out=la, in0=la, scalar1=float(li), scalar2=None, op0=ALU.add))
        ev = P.signal("vector"); P.wait("vector", ev)
        P.emit("vector", lambda e, la=la, nla=nla: e.tensor_scalar(out=nla, in0=la, scalar1=-1.0, scalar2=None, op0=ALU.mult))
        ev = P.signal("vector"); P.wait("vector", ev)
        sl = subln[:, l * 128:(l + 1) * 128]
        P.emit("vector", lambda e, sl=sl, li=lam_init: e.tensor_scalar(out=sl, in0=sl, scalar1=float(1.0 - li), scalar2=None, op0=ALU.mult))
        ev = P.signal("vector"); P.wait("vector", ev)
    P.barrier()

    psb = [nc.alloc_psum_tensor(f"psb{i}", [128, 512], F32).ap() for i in range(6)]
    pst = [nc.alloc_psum_tensor(f"pst{i}", [128, 1024], BF16).ap() for i in range(2)]


    def norm_to_xs(src_hT, gcol0, xs, tok0, ntok, xs_col0, pool):
        hst_ring, sq_ring, rstd_ring, hsem = pool
        NBK = 128
        for (c0, n) in split_blocks(ntok, NBK):
            t0 = tok0 + c0
            i, hst, fr = hst_ring.next()
            P.wait("sync", fr)
            ev_ld = P.dma("sync", hsem[i], hst[:, :, :n], src_hT[:, t0:t0 + n].rearrange("(kt p) n -> p kt n", p=128))
            j, sq, frs = sq_ring.next()
            P.wait("scalar", ev_ld)
            P.wait("scalar", frs)
            P.emit("scalar", lambda e, sq=sq, hst=hst, n=n: e.activation(out=sq[:, :, :n], in_=hst[:, :, :n], func=AF.Square))
            ev_sq = P.signal("scalar")
            pi, ps, frp = ps_ring.next()
            P.wait("tensor", ev_sq)
            P.wait("tensor", frp)
            for kt in range(KT):
                P.emit("tensor", lambda e, ps=ps, sq=sq, kt=kt, n=n: e.matmul(ps[:, :n], lhsT=ones_bf, rhs=sq[:, kt, :n],
                                                                            start=(kt == 0), stop=(kt == KT - 1)))
            ev_mm = P.signal("tensor")
            sq_ring.release(j, ev_mm)
            k, rstd, frr = rstd_ring.next()
            P.wait("vector", ev_mm)
            P.wait("vector", frr)
            P.wait("scalar", ev_mm)
            P.wait("scalar", frr)
            P.emit("scalar", lambda e, rstd=rstd, ps=ps, n=n: e.activation(out=rstd[:, :n], in_=ps[:, :n], func=AF.Sqrt, bias=epsb, scale=1.0 / D))
            ev = P.signal("scalar"); P.wait("vector", ev)
            ps_ring.release(pi, ev)
            P.emit("vector", lambda e, rstd=rstd, n=n: e.reciprocal(out=rstd[:, :n], in_=rstd[:, :n]))
            ev = P.signal("vector"); P.wait("vector", ev)
            if t0 < cfg.NPADF:
                npad = min(cfg.NPADF - t0, n)
                P.emit("vector", lambda e, rstd=rstd, npad=npad: e.memset(rstd[:, :npad], 0.0))
                ev = P.signal("vector"); P.wait("vector", ev)
            P.wait("vector", ev_ld)
            for kt in range(KT):
                eng = "vector"
                o = xs[:, kt, xs_col0 + c0: xs_col0 + c0 + n]
                g = gains[:, gcol0 + kt: gcol0 + kt + 1]
                P.emit(eng, lambda e, o=o, hst=hst, kt=kt, n=n, g=g, rstd=rstd: e.scalar_tensor_tensor(
                    out=o, in0=hst[:, kt, :n], scalar=g, in1=rstd[:, :n], op0=ALU.mult, op1=ALU.mult))
            e1 = P.signal("vector")
            e2 = None
            hst_ring.release(i, [e1, e2])
            rstd_ring.release(k, [e1, e2])
        return [e1, e2]

    def make_norm_pool(tag):
        hst_ring = Ring([sb(f"hst{tag}{i}", [128, KT, 128], F32) for i in range(2)])
        sq_ring = Ring([sb(f"sq{tag}{i}", [128, KT, 128], BF16) for i in range(2)])
        rstd_ring = Ring([sb(f"rstd{tag}{i}", [128, 128], F32) for i in range(2)])
        return (hst_ring, sq_ring, rstd_ring, hsems)

    hsems = [P.newsem(f"hld{i}") for i in range(2)]
    wsems = [P.newsem(f"wld{i}") for i in range(4)]
    ps_ring = Ring(psb[0:4])

    def gemm(xs, nkt, slabs, wring, xs_ready):
        loaded = {}

        def load(si):
            s = slabs[si]
            i, wb, fr = wring.next()
            P.wait("gpsimd", fr)
            ev = P.dma("gpsimd", wsems[i], wb[:, :nkt, :s["w"]], s["src"].rearrange("(kt p) n -> p kt n", p=128))
            loaded[si] = (i, wb, ev)

        load(0)
        if len(slabs) > 1:
            load(1)
        P.wait("tensor", xs_ready)
        for si, s in enumerate(slabs):
            i, wb, ev = loaded.pop(si)
            P.wait("tensor", ev)
            if s["kind"] == "fm":
                for j in range(s["w"] // 128):
                    for bi, (c0, n) in enumerate(s["items"]):
                        pi, ps, frp = ps_ring.next()
                        P.wait("tensor", frp)
                        for kt in range(nkt):
                            P.emit("tensor", lambda e, ps=ps, wb=wb, kt=kt, j=j, c0=c0, n=n: e.matmul(
                                ps[:, :n], lhsT=wb[:, kt, j * 128:(j + 1) * 128], rhs=xs[:, kt, c0:c0 + n],
                                start=(kt == 0), stop=(kt == nkt - 1)))
                        evm = P.signal("tensor")
                        evf = s["handler"](s, j, bi, ps, evm)
                        ps_ring.release(pi, evf)
            else:
                for ti, c0 in enumerate(s["items"]):
                    pi, ps, frp = ps_ring.next()
                    P.wait("tensor", frp)
                    w = s["w"]
                    for kt in range(nkt):
                        P.emit("tensor", lambda e, ps=ps, wb=wb, kt=kt, c0=c0, w=w: e.matmul(
                            ps[:, :w], lhsT=xs[:, kt, c0:c0 + 128], rhs=wb[:, kt, :w],
                            start=(kt == 0), stop=(kt == nkt - 1)))
                    evm = P.signal("tensor")
                    evf = s["handler"](s, 0, ti, ps, evm)
                    ps_ring.release(pi, evf)
            wring.release(i, P.signal("tensor"))
            if si + 2 < len(slabs):
                load(si + 2)

    class Stager:
        def __init__(self, name, shape, dt, n=3):
            self.ring = Ring([sb(f"{name}{i}", shape, dt) for i in range(n)])
            base = name.split("_")[0]
            self.sems = [P.newsem(f"{base}{i}") for i in range(n)]

        def get(self, engs):
            i, buf, fr = self.ring.next()
            for e in engs:
                P.wait(e, fr)
            return i, buf

        def store(self, i, dst, src, ev_ready, q="sync"):
            P.wait(q, ev_ready)
            ev = P.dma(q, self.sems[i], dst, src)
            self.ring.release(i, ev)
            return ev

    def tile_groups(maxtiles):
        out = []
        t = 0
        ng = (NT + maxtiles - 1) // maxtiles
        base = NT // ng
        rem = NT % ng
        for g in range(ng):
            n = base + (1 if g < rem else 0)
            out.append((t, n))
            t += n
        return out

    G_BIG = tile_groups(17)
    G_DOWN = tile_groups(9)
    TGMAX = max(n for _, n in G_BIG) * 128
    TDMAX = max(n for _, n in G_DOWN) * 128
    evac_flip = [0]

    def evac_engine():
        evac_flip[0] ^= 1
        return "scalar" if evac_flip[0] else "vector"

    def copy_op(eng, out, in_, scale=None):
        if eng == "scalar":
            if scale is None:
                P.emit("scalar", lambda e: e.copy(out=out, in_=in_))
            else:
                P.emit("scalar", lambda e: e.mul(out=out, in_=in_, mul=float(scale)))
        else:
            if scale is None:
                P.emit(eng, lambda e: e.tensor_copy(out=out, in_=in_))
            else:
                P.emit(eng, lambda e: e.tensor_scalar(out=out, in0=in_, scalar1=float(scale), scalar2=None, op0=ALU.mult))


    def chain(eng, fn):
        P.emit(eng, fn)
        ev = P.signal(eng)
        P.wait(eng, ev)
        return ev

    def attention_phase(l):
        kT_b = [[sb(f"akT{l}_{i}_{m}", [64, L], BF16) for m in range(2)] for i in range(2)]
        qT_b = [[sb(f"aqT{l}_{i}_{m}", [64, L], BF16) for m in range(2)] for i in range(2)]
        v_b = [sb(f"avv{l}_{i}", [128, NT, 129], BF16) for i in range(2)]
        pT_ring = Ring([sb(f"apT{l}_{i}", [128, 256], BF16) for i in range(4)])
        s_ring = Ring([psb[4][:, 0:256], psb[5][:, 0:256]])
        rec = sb(f"arec{l}", [128, 4], F32)
        t1 = sb(f"at1{l}", [128, 128], F32)
        o_sb = sb(f"ao{l}", [128, 128], F32)
        sqb = sb(f"asq{l}", [128, 128], F32)
        ss = sb(f"ass{l}", [128, 2], F32)
        yst = Stager(f"ya_{l}_", [128, 128], BF16, 3)
        lsem = [P.newsem(f"ald{i}") for i in range(2)]
        hfree = [None, None]
        ofree = [None, None]
        comb_done = None
        evs1 = []
        for i in range(2):
            P.emit("gpsimd", lambda e, i=i: e.memset(v_b[i][:, :, 128:129], 1.0))
            evs1.append(P.signal("gpsimd"))
        loads = {}

        def load_head(h):
            b = h % 2
            P.wait("sync", hfree[b])
            for m in range(2):
                P.dma("sync", lsem[b], kT_b[b][m], kaT[h * 128 + m * 64:h * 128 + (m + 1) * 64, :])
                P.dma("sync", lsem[b], qT_b[b][m], qaT[h * 128 + m * 64:h * 128 + (m + 1) * 64, :])
            ev = P.dma("sync", lsem[b], v_b[b][:, :, 0:128], va[:, h * 128:(h + 1) * 128].rearrange("(t p) e -> p t e", p=128))
            loads[h] = ev

        load_head(0)
        cnt = 0
        for h in range(AH):
            if h + 1 < AH:
                load_head(h + 1)
            b = h % 2
            P.wait("tensor", loads[h])
            P.wait("tensor", evs1)
            kT, qT, vv = kT_b[b], qT_b[b], v_b[b]
            for n in range(NT):
                par = cnt % 2
                cnt += 1
                O = [psb[2 * par + m] for m in range(2)]
                P.wait("tensor", ofree[par])
                pend = []

                def issue_pv(ent, O=O, vv=vv, n=n):
                    kt2, pT2, pi2, evp2 = ent
                    P.wait("tensor", evp2)
                    for m2 in range(2):
                        P.emit("tensor", lambda e, O=O, m2=m2, kt2=kt2, pT2=pT2, vv=vv, n=n: e.matmul(
                            O[m2][:, :129], lhsT=pT2[:, m2 * 128:(m2 + 1) * 128], rhs=vv[:, kt2, :], start=(kt2 == 0), stop=(kt2 == n)))
                    pT_ring.release(pi2, P.signal("tensor"))

                for kt in range(n + 1):
                    si, ps_s, frs = s_ring.next()
                    P.wait("tensor", frs)
                    for m in range(2):
                        P.emit("tensor", lambda e, ps_s=ps_s, kT=kT, qT=qT, m=m, kt=kt, n=n: e.matmul(
                            ps_s[:, m * 128:(m + 1) * 128], lhsT=kT[m][:, kt * 128:(kt + 1) * 128],
                            rhs=qT[m][:, n * 128:(n + 1) * 128], start=True, stop=True))
                    ev_s = P.signal("tensor")
                    pi, pT, frp = pT_ring.next()
                    P.wait("scalar", ev_s)
                    P.wait("scalar", frp)
                    col = ((1 if kt == 0 else 0) * AH + h) * NT + (n - kt)
                    bias = biasT[:, col:col + 1]
                    P.emit("scalar", lambda e, pT=pT, ps_s=ps_s, bias=bias: e.activation(out=pT, in_=ps_s, func=AF.Exp, bias=bias, scale=1.0))
                    ev_p = P.signal("scalar")
                    s_ring.release(si, ev_p)
                    if kt == n:
                        P.wait("gpsimd", ev_p)
                        P.emit("gpsimd", lambda e, pT=pT: e.tensor_tensor(out=pT, in0=pT, in1=tri, op=ALU.mult))
                        ev_p = P.signal("gpsimd")
                    pend.append((kt, pT, pi, ev_p))
                    if len(pend) > 1:
                        issue_pv(pend.pop(0))
                while pend:
                    issue_pv(pend.pop(0))
                ev_o = P.signal("tensor")
                P.wait("vector", ev_o)
                for m in range(2):
                    P.emit("vector", lambda e, O=O, m=m: e.tensor_scalar(out=rec[:, m:m + 1], in0=O[m][:, 128:129], scalar1=1e-30, scalar2=None, op0=ALU.add))
                chain("vector", lambda e: e.reciprocal(out=rec[:, 0:2], in_=rec[:, 0:2])) if False else None
                ev = P.signal("vector"); P.wait("vector", ev)
                chain("vector", lambda e: e.reciprocal(out=rec[:, 2:4], in_=rec[:, 0:2]))
                nlam = lam_t[:, 4 * l + 1:4 * l + 2]
                chain("vector", lambda e, nlam=nlam: e.tensor_tensor(out=rec[:, 1:2], in0=rec[:, 3:4], in1=nlam, op=ALU.mult))
                chain("vector", lambda e, O=O: e.tensor_scalar(out=t1, in0=O[1][:, 0:128], scalar1=rec[:, 1:2], scalar2=None, op0=ALU.mult))
                chain("vector", lambda e, O=O: e.scalar_tensor_tensor(out=o_sb, in0=O[0][:, 0:128], scalar=rec[:, 2:3], in1=t1, op0=ALU.mult, op1=ALU.add))
                ofree[par] = P.signal("vector")
                chain("vector", lambda e: e.tensor_tensor(out=sqb, in0=o_sb, in1=o_sb, op=ALU.mult))
                chain("vector", lambda e: e.reduce_sum(out=ss[:, 0:1], in_=sqb, axis=mybir.AxisListType.X))
                P.wait("scalar", P.signal("vector"))
                P.emit("scalar", lambda e: e.activation(out=ss[:, 1:2], in_=ss[:, 0:1], func=AF.Sqrt, bias=epsb, scale=1.0 / 128))
                P.wait("vector", P.signal("scalar"))
                chain("vector", lambda e: e.reciprocal(out=ss[:, 1:2], in_=ss[:, 1:2]))
                yi, yb = yst.get(["vector"])
                sl = subln[:, l * 128:(l + 1) * 128]
                P.emit("vector", lambda e, yb=yb, sl=sl: e.scalar_tensor_tensor(out=yb, in0=o_sb, scalar=ss[:, 1:2], in1=sl, op0=ALU.mult, op1=ALU.mult))
                ev_y = P.signal("vector")
                P.wait("vector", ev_y)
                yst.store(yi, ytm[n * 128:(n + 1) * 128, h * 128:(h + 1) * 128], yb, ev_y)
            hfree[b] = P.signal("tensor")

    def retention_phase(l):
        NB = 2
        rq_b = [sb(f"rq{l}_{i}", [128, RH, 128], BF16) for i in range(NB)]
        rk_b = [sb(f"rk{l}_{i}", [128, RH, 128], BF16) for i in range(NB)]
        rkd_b = [sb(f"rkd{l}_{i}", [128, RH * 128], BF16) for i in range(NB)]
        rv_b = [sb(f"rv{l}_{i}", [128, DR], BF16) for i in range(NB)]
        rg_b = [sb(f"rg{l}_{i}", [128, DR], F32) for i in range(NB)]
        S_f = sb(f"Sf{l}", [128, RH, 256], F32)
        S_b = sb(f"Sb{l}", [128, RH, 256], BF16)
        sT_ring = Ring([sb(f"rsT{l}_{i}", [128, 128], BF16) for i in range(2)])
        o_sb = sb(f"ro{l}", [128, 256], F32)
        sqb = sb(f"rsq{l}", [128, 256], F32)
        tb = sb(f"rt{l}", [128, 256], F32)
        ss = sb(f"rss{l}", [128, 2], F32)
        yst = Stager(f"yr_{l}_", [128, 256], BF16, 3)
        lsem = [P.newsem(f"rld{i}") for i in range(NB)]
        bfree = [None] * NB
        P.emit("vector", lambda e: e.memset(S_f, 0.0))
        ev_a = P.signal("vector")
        P.emit("gpsimd", lambda e: e.memset(S_b, 0.0))
        ev_b = P.signal("gpsimd")
        P.wait("vector", ev_b); P.wait("tensor", ev_b); P.wait("tensor", ev_a); P.wait("vector", ev_a)
        s_slots = Ring([psb[4][:, 0:128], psb[5][:, 0:128]])
        o_ps = Ring([psb[0][:, :256], psb[1][:, :256]])
        kv_ps = Ring([psb[2][:, :256], psb[3][:, :256]])
        loads = {}

        def load(n):
            b = n % NB
            P.wait("sync", bfree[b])
            t0 = n * 128
            P.dma("sync", lsem[b], rq_b[b], rqT[:, t0:t0 + 128].rearrange("(r p) t -> p r t", p=128))
            P.dma("sync", lsem[b], rk_b[b], rkT[:, t0:t0 + 128].rearrange("(r p) t -> p r t", p=128))
            P.dma("sync", lsem[b], rkd_b[b], rkd[t0:t0 + 128, :])
            P.dma("sync", lsem[b], rv_b[b], rv[t0:t0 + 128, :])
            loads[n] = P.dma("sync", lsem[b], rg_b[b], rgs[t0:t0 + 128, :])

        load(0)
        sb_ready = [None] * RH
        for n in range(NT):
            if n + 1 < NT:
                load(n + 1)
            b = n % NB
            P.wait("tensor", loads[n])
            P.wait("vector", loads[n])
            for r in range(RH):
                si, ps_s, frs = s_slots.next()
                P.wait("tensor", frs)
                P.emit("tensor", lambda e, ps_s=ps_s, b=b, r=r: e.matmul(ps_s, lhsT=rk_b[b][:, r, :], rhs=rq_b[b][:, r, :], start=True, stop=True))
                ev_s = P.signal("tensor")
                ti, sT, frt = sT_ring.next()
                P.wait("vector", ev_s); P.wait("vector", frt)
                P.emit("vector", lambda e, sT=sT, ps_s=ps_s, r=r: e.tensor_tensor(out=sT, in0=ps_s, in1=retM[:, r * 128:(r + 1) * 128], op=ALU.mult))
                ev_m = P.signal("vector")
                s_slots.release(si, ev_m)
                oi, po, fro = o_ps.next()
                P.wait("tensor", ev_m); P.wait("tensor", fro); P.wait("tensor", sb_ready[r])
                P.emit("tensor", lambda e, po=po, sT=sT, b=b, r=r: e.matmul(po, lhsT=sT, rhs=rv_b[b][:, r * 256:(r + 1) * 256], start=True, stop=False))
                P.emit("tensor", lambda e, po=po, b=b, r=r: e.matmul(po, lhsT=rq_b[b][:, r, :], rhs=S_b[:, r, :], start=False, stop=True))
                ev_o = P.signal("tensor")
                sT_ring.release(ti, ev_o)
                ki, pk, frk = kv_ps.next()
                P.wait("tensor", frk)
                P.emit("tensor", lambda e, pk=pk, b=b, r=r: e.matmul(pk, lhsT=rkd_b[b][:, r * 128:(r + 1) * 128], rhs=rv_b[b][:, r * 256:(r + 1) * 256], start=True, stop=True))
                ev_k = P.signal("tensor")
                P.wait("vector", ev_k); P.wait("vector", ev_o)
                chain("vector", lambda e, pk=pk, r=r: e.scalar_tensor_tensor(out=S_f[:, r, :], in0=S_f[:, r, :], scalar=gC[:, r:r + 1], in1=pk, op0=ALU.mult, op1=ALU.add))
                kv_ps.release(ki, P.signal("vector"))
                P.emit("vector", lambda e, r=r: e.tensor_copy(out=S_b[:, r, :], in_=S_f[:, r, :]))
                sb_ready[r] = P.signal("vector")
                P.wait("vector", sb_ready[r])
                chain("vector", lambda e, po=po: e.tensor_copy(out=o_sb, in_=po))
                o_ps.release(oi, P.signal("vector"))
                chain("vector", lambda e: e.tensor_tensor(out=sqb, in0=o_sb, in1=o_sb, op=ALU.mult))
                chain("vector", lambda e: e.reduce_sum(out=ss[:, 0:1], in_=sqb, axis=mybir.AxisListType.X))
                P.wait("scalar", P.signal("vector"))
                P.emit("scalar", lambda e: e.activation(out=ss[:, 1:2], in_=ss[:, 0:1], func=AF.Sqrt, bias=epsb, scale=1.0 / 256))
                P.wait("vector", P.signal("scalar"))
                chain("vector", lambda e: e.reciprocal(out=ss[:, 1:2], in_=ss[:, 1:2]))
                rn = retn[:, l * 256:(l + 1) * 256]
                chain("vector", lambda e, rn=rn: e.scalar_tensor_tensor(out=tb, in0=o_sb, scalar=ss[:, 1:2], in1=rn, op0=ALU.mult, op1=ALU.mult))
                yi, yb = yst.get(["vector"])
                P.emit("vector", lambda e, yb=yb, b=b, r=r: e.tensor_tensor(out=yb, in0=tb, in1=rg_b[b][:, r * 256:(r + 1) * 256], op=ALU.mult))
                ev_y = P.signal("vector")
                P.wait("vector", ev_y)
                yst.store(yi, ytm[n * 128:(n + 1) * 128, DA + r * 256:DA + (r + 1) * 256], yb, ev_y)
            bfree[b] = [P.signal("tensor"), P.signal("vector")]

    def transpose_phase(l):
        y_b = [sb(f"ty{l}_{i}", [128, DMIX], BF16) for i in range(2)]
        lsem = [P.newsem(f"tld{i}") for i in range(2)]
        yst = Stager(f"tyT_{l}_", [128, MT, 128], BF16, 2)
        bfree = [None, None]
        pfree = [None, None]
        loads = {}

        def load(n):
            b = n % 2
            P.wait("sync", bfree[b])
            loads[n] = P.dma("sync", lsem[b], y_b[b], ytm[n * 128:(n + 1) * 128, :])

        load(0)
        q = 0
        for n in range(NT):
            if n + 1 < NT:
                load(n + 1)
            b = n % 2
            P.wait("tensor", loads[n])
            yi, yb = yst.get(["vector", "scalar"])
            evs = []
            for (c0, nc_) in split_blocks(MT, 4):
                half = q % 2
                q += 1
                P.wait("tensor", pfree[half])
                for c in range(c0, c0 + nc_):
                    dst = pst[half][:, (c - c0) * 128: (c - c0 + 1) * 128]
                    P.emit("tensor", lambda e, dst=dst, b=b, c=c: e.transpose(out=dst, in_=y_b[b][:, c * 128:(c + 1) * 128], identity=ident))
                ev_t = P.signal("tensor")
                eng = evac_engine()
                P.wait(eng, ev_t)
                src = pst[half][:, 0: nc_ * 128]
                dstv = yb[:, c0:c0 + nc_, :]
                copy_op(eng, dstv, src.rearrange("p (c t) -> p c t", t=128))
                pfree[half] = P.signal(eng)
                evs.append(pfree[half])
            bfree[b] = P.signal("tensor")
            yst.store(yi, yT[:, n * 128:(n + 1) * 128].rearrange("(c p) t -> p c t", p=128), yb, evs)

    def ckpt(name):
        if stop == name:
            raise StopBuild()

    def build_layers():
        ckpt("const")
        for l in range(DEPTH):
            h_src = h0T if l == 0 else hT
            P.new_epoch()
            phase_begin()
            xs = sb(f"xs1_{l}", [128, KT, TGMAX], BF16)
            wring = Ring([sb(f"w1_{l}_{i}", [128, KT, 512], BF16) for i in range(2)])
            npool = make_norm_pool(f"n1_{l}_")
            st_bf = Stager(f"s1b_{l}_", [128, 512], BF16, 4)
            st_f = Stager(f"s1f_{l}_", [128, 512], F32, 2)
            for (gt0, gnt) in G_BIG:
                tok0, ntok = gt0 * 128, gnt * 128
                xs_ready = norm_to_xs(h_src, l * KT, xs, tok0, ntok, 0, npool)
                blocks = split_blocks(ntok, 512)
                tiles = [t * 128 for t in range(gnt)]

                def h_fm(s, j, bi, ps, evm, tok0=tok0, blocks=blocks):
                    c0, n = blocks[bi]
                    eng = evac_engine() if s.get("mul") is None else "vector"
                    i, buf = st_bf.get([eng])
                    P.wait(eng, evm)
                    if s.get("mul") is not None:
                        m = s["mul"](j)
                        P.emit("vector", lambda e, buf=buf, ps=ps, n=n, m=m: e.tensor_tensor(out=buf[:, :n], in0=ps[:, :n], in1=m[:, :n], op=ALU.mult))
                    else:
                        copy_op(eng, buf[:, :n], ps[:, :n], s.get("scale"))
                    ev = P.signal(eng)
                    r0 = s["row0"] + j * 128
                    st_bf.store(i, s["dst"][r0:r0 + 128, tok0 + c0: tok0 + c0 + n], buf[:, :n], ev)
                    return ev

                def h_tm(s, j, ti, ps, evm, tok0=tok0):
                    w = s["w"]
                    if s.get("silu"):
                        i, buf = st_f.get(["scalar"])
                        P.wait("scalar", evm)
                        P.emit("scalar", lambda e, buf=buf, ps=ps, w=w: e.activation(out=buf[:, :w], in_=ps[:, :w], func=AF.Silu))
                        ev = P.signal("scalar")
                        st_f.store(i, s["dst"][tok0 + ti * 128: tok0 + (ti + 1) * 128, s["col0"]:s["col0"] + w], buf[:, :w], ev)
                        return ev
                    eng = evac_engine() if s.get("mulfull") is None else "vector"
                    i, buf = st_bf.get([eng])
                    P.wait(eng, evm)
                    if s.get("mulfull") is not None:
                        m = s["mulfull"]
                        P.emit("vector", lambda e, buf=buf, ps=ps, w=w, m=m: e.tensor_tensor(out=buf[:, :w], in0=ps[:, :w], in1=m[:, :w], op=ALU.mult))
                    else:
                        copy_op(eng, buf[:, :w], ps[:, :w])
                    ev = P.signal(eng)
                    st_bf.store(i, s["dst"][tok0 + ti * 128: tok0 + (ti + 1) * 128, s["col0"]:s["col0"] + w], buf[:, :w], ev)
                    return ev

                slabs = []

                def add_fm(c_start, width, dst, **kw):
                    for (o, w) in split_blocks(width, 512):
                        slabs.append(dict(src=w_in[l, :, c_start + o:c_start + o + w], w=w, kind="fm", items=blocks,
                                          handler=h_fm, dst=dst, row0=o, **kw))

                def add_tm(c_start, width, dst, **kw):
                    for (o, w) in split_blocks(width, 512):
                        slabs.append(dict(src=w_in[l, :, c_start + o:c_start + o + w], w=w, kind="tm", items=tiles,
                                          handler=h_tm, dst=dst, col0=o, **kw))

                add_fm(cfg.c_aq, DA, qaT, scale=0.125)
                add_fm(cfg.c_ak, DA, kaT)
                add_tm(cfg.c_av, DA, va)
                for (o, w) in split_blocks(RH * 128, 512):
                    slabs.append(dict(src=w_in[l, :, cfg.c_rq + o:cfg.c_rq + o + w], w=w, kind="fm", items=blocks, handler=h_fm,
                                      dst=rqT, row0=o, mul=(lambda j, o=o: gq[:, (o // 128 + j) * 512:(o // 128 + j + 1) * 512])))
                add_fm(cfg.c_rk, RH * 128, rkT)
                for (o, w) in split_blocks(RH * 128, 512):
                    slabs.append(dict(src=w_in[l, :, cfg.c_rk + o:cfg.c_rk + o + w], w=w, kind="tm", items=tiles, handler=h_tm,
                                      dst=rkd, col0=o, mulfull=kdec[:, o:o + w]))
                add_tm(cfg.c_rv, DR, rv)
                add_tm(cfg.c_rg, DR, rgs, silu=True)
                gemm(xs, KT, slabs, wring, xs_ready)
            phase_end()
            ckpt(f"p1_{l}")
            phase_begin()
            attention_phase(l)
            phase_end()
            ckpt(f"att_{l}")
            phase_begin()
            retention_phase(l)
            phase_end()
            ckpt(f"ret_{l}")
            phase_begin()
            transpose_phase(l)
            phase_end()
            ckpt(f"tr_{l}")
            phase_begin()
            xs4 = sb(f"xs4_{l}", [128, MT, TGMAX], BF16)
            wring4 = Ring([sb(f"w4_{l}_{i}", [128, MT, 512], BF16) for i in range(2)])
            hrsem = [P.newsem(f"hr{i}") for i in range(2)]
            xsem = P.newsem("xs4")

            def residual_gemm(xs_t, nkt, src_act, wsrc, wring_t, groups, width_slab, tag):
                hrow = Stager(f"hrow{tag}_{l}_", [128, max(n for _, n in groups) * 128], F32, 2)
                for (gt0, gnt) in groups:
                    tok0, ntok = gt0 * 128, gnt * 128
                    xs_ready = P.dma("sync", xsem, xs_t[:, :, :ntok], src_act[:, tok0:tok0 + ntok].rearrange("(kt p) n -> p kt n", p=128))
                    blocks = split_blocks(ntok, 512)
                    state = {}

                    def h_res(s, j, bi, ps, evm, tok0=tok0, ntok=ntok, blocks=blocks, state=state):
                        c0, n = blocks[bi]
                        r0 = s["row0"] + j * 128
                        if bi == 0:
                            i, buf = hrow.get(["sync"])
                            evl = P.dma("sync", hrsem[i], buf[:, :ntok], (h0T if (l == 0 and s["first"]) else hT)[r0:r0 + 128, tok0:tok0 + ntok])
                            state["cur"] = (i, buf, evl)
                        i, buf, evl = state["cur"]
                        P.wait("vector", evl)
                        P.wait("vector", evm)
                        P.emit("vector", lambda e, buf=buf, ps=ps, c0=c0, n=n: e.tensor_tensor(out=buf[:, c0:c0 + n], in0=ps[:, :n], in1=buf[:, c0:c0 + n], op=ALU.add))
                        ev = P.signal("vector")
                        if bi == len(blocks) - 1:
                            hrow.store(i, hT[r0:r0 + 128, tok0:tok0 + ntok], buf[:, :ntok], ev)
                        return ev

                    slabs = []
                    for (o, w) in split_blocks(D, width_slab):
                        slabs.append(dict(src=wsrc[:, o:o + w], w=w, kind="fm", items=blocks, handler=h_res, row0=o, first=s_first[0]))
                    gemm(xs_t, nkt, slabs, wring_t, xs_ready)
                    P.wait("sync", P.signal("tensor"))

            s_first = [True]
            residual_gemm(xs4, MT, yT, w_out[l], wring4, G_BIG, 512, "a")
            phase_end()
            ckpt(f"p4_{l}")
            phase_begin()
            xs5 = sb(f"xs5_{l}", [128, KT, TGMAX + 2], BF16)
            wg_ring = Ring([sb(f"wg_{l}_{i}", [128, KT, 512], BF16) for i in range(2)])
            wu_ring = Ring([sb(f"wu_{l}_{i}", [128, KT, 512], BF16) for i in range(2)])
            npool5 = make_norm_pool(f"n5_{l}_")
            cst = Stager(f"cv_{l}_", [128, 512], F32, 2)
            sgst = Stager(f"sg_{l}_", [128, 512], F32, 2)
            ast = Stager(f"a_{l}_", [128, 512], BF16, 3)
            wgsems = [P.newsem(f"wg{i}") for i in range(2)]
            wusems = [P.newsem(f"wu{i}") for i in range(2)]
            for (gt0, gnt) in G_BIG:
                tok0, ntok = gt0 * 128, gnt * 128
                if tok0 == 0:
                    P.emit("vector", lambda e, xs5=xs5: e.memset(xs5[:, :, 0:2], 0.0))
                    ev0 = P.signal("vector")
                    xs_ready = norm_to_xs(hT, (DEPTH + l) * KT, xs5, 0, ntok, 2, npool5) + [ev0]
                else:
                    xs_ready = norm_to_xs(hT, (DEPTH + l) * KT, xs5, tok0 - 2, ntok + 2, 0, npool5)
                blocks = split_blocks(ntok, 510)
                P.wait("tensor", xs_ready)
                loaded = {}

                def load5(fi):
                    i, wgb, frg = wg_ring.next()
                    _, wub, fru = wu_ring.next()
                    P.wait("gpsimd", frg)
                    P.wait("gpsimd", fru)
                    w = min(512, DFF - fi * 512)
                    e1 = P.dma("gpsimd", wgsems[i], wgb[:, :, :w], w_gate[l, :, fi * 512:fi * 512 + w].rearrange("(kt p) n -> p kt n", p=128))
                    e2 = P.dma("gpsimd", wusems[i], wub[:, :, :w], w_up[l, :, fi * 512:fi * 512 + w].rearrange("(kt p) n -> p kt n", p=128))
                    loaded[fi] = (i, wgb, wub, w, [e1, e2])

                nfs = (DFF + 511) // 512
                load5(0)
                if nfs > 1:
                    load5(1)
                for fi in range(nfs):
                    i, wgb, wub, w, evw = loaded.pop(fi)
                    P.wait("tensor", evw)
                    for j in range(w // 128):
                        f = fi * 4 + j
                        cw0 = convw[:, (l * 3 + 0) * FT + f:(l * 3 + 0) * FT + f + 1]
                        cw1 = convw[:, (l * 3 + 1) * FT + f:(l * 3 + 1) * FT + f + 1]
                        cw2 = convw[:, (l * 3 + 2) * FT + f:(l * 3 + 2) * FT + f + 1]
                        for (c0, n) in blocks:
                            pg_i, pg, frg = ps_ring.next()
                            P.wait("tensor", frg)
                            for kt in range(KT):
                                P.emit("tensor", lambda e, pg=pg, wgb=wgb, kt=kt, j=j, c0=c0, n=n, xs5=xs5: e.matmul(
                                    pg[:, :n + 2], lhsT=wgb[:, kt, j * 128:(j + 1) * 128], rhs=xs5[:, kt, c0:c0 + n + 2],
                                    start=(kt == 0), stop=(kt == KT - 1)))
                            ev_g = P.signal("tensor")
                            pu_i, pu, fru = ps_ring.next()
                            P.wait("tensor", fru)
                            for kt in range(KT):
                                P.emit("tensor", lambda e, pu=pu, wub=wub, kt=kt, j=j, c0=c0, n=n, xs5=xs5: e.matmul(
                                    pu[:, :n], lhsT=wub[:, kt, j * 128:(j + 1) * 128], rhs=xs5[:, kt, c0 + 2:c0 + 2 + n],
                                    start=(kt == 0), stop=(kt == KT - 1)))
                            ev_u = P.signal("tensor")
                            ci, cb = cst.get(["vector"])
                            P.wait("vector", ev_g)
                            P.emit("vector", lambda e, cb=cb, pg=pg, n=n, cw2=cw2: e.tensor_scalar(out=cb[:, :n], in0=pg[:, 2:n + 2], scalar1=cw2, scalar2=None, op0=ALU.mult))
                            ev = P.signal("vector"); P.wait("vector", ev)
                            P.emit("vector", lambda e, cb=cb, pg=pg, n=n, cw1=cw1: e.scalar_tensor_tensor(out=cb[:, :n], in0=pg[:, 1:n + 1], scalar=cw1, in1=cb[:, :n], op0=ALU.mult, op1=ALU.add))
                            ev = P.signal("vector"); P.wait("vector", ev)
                            P.emit("vector", lambda e, cb=cb, pg=pg, n=n, cw0=cw0: e.scalar_tensor_tensor(out=cb[:, :n], in0=pg[:, 0:n], scalar=cw0, in1=cb[:, :n], op0=ALU.mult, op1=ALU.add))
                            ev_c = P.signal("vector")
                            ps_ring.release(pg_i, ev_c)
                            si_, sgb = sgst.get(["scalar"])
                            P.wait("scalar", ev_c)
                            P.emit("scalar", lambda e, sgb=sgb, cb=cb, n=n: e.activation(out=sgb[:, :n], in_=cb[:, :n], func=AF.Silu))
                            ev_s = P.signal("scalar")
                            cst.ring.release(ci, ev_s)
                            ai, ab = ast.get(["vector"])
                            P.wait("vector", ev_s)
                            P.wait("vector", ev_u)
                            P.emit("vector", lambda e, ab=ab, sgb=sgb, pu=pu, n=n: e.tensor_tensor(out=ab[:, :n], in0=pu[:, :n], in1=sgb[:, :n], op=ALU.mult))
                            ev_a = P.signal("vector")
                            ps_ring.release(pu_i, ev_a)
                            sgst.ring.release(si_, ev_a)
                            ast.store(ai, aT[f * 128:(f + 1) * 128, tok0 + c0:tok0 + c0 + n], ab[:, :n], ev_a)
                    ev_t = P.signal("tensor")
                    wg_ring.release(i, ev_t)
                    wu_ring.release(i, ev_t)
                    if fi + 2 < nfs:
                        load5(fi + 2)
            phase_end()
            ckpt(f"p5_{l}")
            phase_begin()
            xs6 = sb(f"xs6_{l}", [128, FT, TDMAX], BF16)
            wring6 = Ring([sb(f"w6_{l}_{i}", [128, FT, 256], BF16) for i in range(2)])
            s_first[0] = False
            residual_gemm(xs6, FT, aT, w_down[l], wring6, G_DOWN, 256, "b")
            phase_end()
            ckpt(f"p6_{l}")

        phase_begin()
        npoolf = make_norm_pool("nf_")
        xsf = sb("xsf", [128, KT, 512], F32)
        fsem = P.newsem("fin")
        last = None
        for (c0, n) in split_blocks(L - 128, 512):
            P.wait("vector", last); P.wait("gpsimd", last)
            evx = norm_to_xs(hT, 2 * DEPTH * KT, xsf, 128 + c0, n, 0, npoolf)
            P.wait("sync", evx)
            last = P.dma("sync", fsem, outT[:, c0:c0 + n].rearrange("(kt p) n -> p kt n", p=128), xsf[:, :, :n])
        P.wait("sync", last)
        phase_end()

    try:
        build_layers()
    except StopBuild:
        P.barrier()
    P.run()
    while stacks:
        stacks.pop().close()
    return nc


def make_consts(cfg):
    AH, RH, NT = cfg.AH, cfg.RH, cfg.NT
    p = np.arange(128, dtype=np.float64)
    c = {}
    c["ident"] = np.eye(128, dtype=np.float32)
    c["tri"] = np.tile((p[None, :] >= p[:, None]).astype(np.float32), (1, 2))
    bt = np.zeros((128, 2, AH, NT), np.float64)
    for h in range(AH):
        sl = cfg.slopes[h]
        for d in range(NT):
            v = sl * (p - 64.0 - 128.0 * d)
            bt[:, 0, h, d] = v
            vm = v.copy()
            vm[: cfg.NPADF] = -30000.0
            bt[:, 1, h, d] = vm
    c["biasT"] = bt.reshape(128, -1).astype(np.float32)
    retM = np.zeros((128, RH, 128), np.float64)
    gq = np.zeros((128, RH, 512), np.float64)
    kdec = np.zeros((128, RH, 128), np.float64)
    gC = np.zeros((128, RH), np.float64)
    for r in range(RH):
        lg = cfg.log_g[r]
        retM[:, r, :] = (128 ** -0.5) * np.exp(-lg * (p[:, None] + 1.0)) * (p[None, :] >= p[:, None])
        gq[:, r, :] = np.exp(lg * ((np.arange(512) % 128) + 1.0))[None, :]
        kdec[:, r, :] = ((128 ** -0.5) * np.exp(lg * (127.0 - p)))[:, None]
        gC[:, r] = np.exp(lg * 128.0)
    c["retM"] = retM.reshape(128, -1).astype(np.float32)
    c["gq"] = gq.reshape(128, -1).astype(np.float32)
    c["kdec"] = kdec.reshape(128, -1).astype(np.float32)
    c["gC"] = gC.astype(np.float32)
    return c


def feat_major(v, nt):
    v = np.asarray(v, np.float32)
    lead = v.shape[:-1]
    v = v.reshape(lead + (nt, 128))
    v = np.moveaxis(v, -1, 0)
    return np.ascontiguousarray(v.reshape(128, -1))


def host_inputs(cfg, x, meta_tokens, attn_norm, w_in, lambda_qk, attn_subln, ret_norm, w_out,
                ffn_norm, w_gate, w_up, conv_w, w_down, final_norm):
    KT, FT, DEPTH = cfg.KT, cfg.FT, cfg.DEPTH
    shared = dict(make_consts(cfg))
    g = np.concatenate([np.asarray(attn_norm, np.float32), np.asarray(ffn_norm, np.float32),
                        np.asarray(final_norm, np.float32)[None]], axis=0)
    shared["gains"] = feat_major(g, KT)
    shared["convw"] = feat_major(np.asarray(conv_w, np.float32), FT)
    rep = lambda a: np.ascontiguousarray(np.broadcast_to(np.asarray(a, np.float32).reshape(1, -1), (128, np.asarray(a).size)))
    shared["subln"] = rep(attn_subln)
    shared["retn"] = rep(ret_norm)
    shared["lamqk"] = rep(lambda_qk)
    for k, v in [("w_in", w_in), ("w_out", w_out), ("w_gate", w_gate), ("w_up", w_up), ("w_down", w_down)]:
        shared[k] = np.ascontiguousarray(np.asarray(v, np.float32))
    x = np.asarray(x, np.float32)
    meta = np.asarray(meta_tokens, np.float32)
    B = x.shape[0]
    maps = []
    npad = cfg.NPADF
    for b in range(B):
        h0 = np.concatenate([np.zeros((npad, cfg.D), np.float32), meta, x[b]], axis=0)
        m = dict(shared)
        m["h0T"] = np.ascontiguousarray(h0.T)
        maps.append(m)
    return maps


_CACHE = {}


def run_cfg(cfg, key, stop=None, **inputs):
    if key not in _CACHE:
        _CACHE[key] = build_program(cfg, stop)
    nc = _CACHE[key]
    maps = host_inputs(cfg, **inputs)
    res = run_bass_kernel_spmd(nc, maps, core_ids=list(range(len(maps))))
    outs = [np.ascontiguousarray(r["outT"].T) for r in res.results]
    return np.stack(outs, axis=0).astype(np.float32)


def kernel(**inputs):
    cfg = Cfg()
    return run_cfg(cfg, "full", **inputs)
```
